# Optimizing a Trainium2 kernel written in Bass

```python
import math
import jax, jax.numpy as jnp
from jax import lax
import numpy as np

D_MODEL = 1024
BATCH = 4
SEQ = 4096
DEPTH = 2

HEAD_DIM = 64
GROUP_W = D_MODEL // 4
MIX_W = 4 * GROUP_W
DIFF_HEADS = GROUP_W // HEAD_DIM
DIFF_QK_DIM = HEAD_DIM // 2
CONV_W = GROUP_W
CONV_KERNEL = 31
GMLP_HEADS = GROUP_W // HEAD_DIM
GMLP_W = GROUP_W
CHUNK = 128
FOX_HEADS = GROUP_W // HEAD_DIM
FOX_W = GROUP_W
D_FF = ((8 * D_MODEL // 3 + 127) // 128) * 128
Q_BLOCK = 128
N_MOD = 9
EPS = 1e-6
NEG_INF = -1e30

DIFF_Q0 = 0
DIFF_K0 = DIFF_Q0 + DIFF_HEADS * 2 * DIFF_QK_DIM
DIFF_V0 = DIFF_K0 + DIFF_HEADS * 2 * DIFF_QK_DIM
CONV0 = DIFF_V0 + DIFF_HEADS * HEAD_DIM
GMLP0 = CONV0 + 2 * CONV_W
FOX_Q0 = GMLP0 + 2 * GMLP_W
FOX_K0 = FOX_Q0 + FOX_W
FOX_V0 = FOX_K0 + FOX_W
FOX_F0 = FOX_V0 + FOX_W
IN_COLS = FOX_F0 + FOX_HEADS

kernel_name = "hybrid_parallel_group_decoder"


def _rms_norm(x, g):
    xf = x.astype(jnp.float32)
    y = xf * lax.rsqrt(jnp.mean(xf * xf, axis=-1, keepdims=True) + EPS)
    return (y * g.astype(jnp.float32)).astype(x.dtype)


def _layer_norm(x, g, b):
    xf = x.astype(jnp.float32)
    mu = jnp.mean(xf, axis=-1, keepdims=True)
    var = jnp.mean(jnp.square(xf - mu), axis=-1, keepdims=True)
    y = (xf - mu) * lax.rsqrt(var + EPS)
    return (y * g.astype(jnp.float32) + b.astype(jnp.float32)).astype(x.dtype)


def _modulate(xn, shift, scale):
    return xn * (1 + scale[:, None, :]) + shift[:, None, :]


def _swiglu(x, w_in, w_out):
    gate, up = jnp.split(x @ w_in, 2, axis=-1)
    return (jax.nn.silu(gate) * up) @ w_out


def _differential_attention(q, k, v, lam, lam_init, out_g):
    B, S, H, _, dq = q.shape
    dv = v.shape[-1]
    nb = S // Q_BLOCK
    q_blocks = jnp.moveaxis(q.reshape(B, nb, Q_BLOCK, H, 2, dq), 1, 0)
    k_pos = jnp.arange(S)
    scale = dq ** -0.5

    def one_block(args):
        qi, i = args
        s = jnp.einsum('bqhcd,bkhcd->bhcqk', qi, k).astype(jnp.float32) * scale
        q_pos = i * Q_BLOCK + jnp.arange(Q_BLOCK)
        causal = k_pos[None, :] <= q_pos[:, None]
        p = jax.nn.softmax(jnp.where(causal, s, NEG_INF), axis=-1)
        a = p[:, :, 0] - lam * p[:, :, 1]
        return jnp.einsum('bhqk,bkhe->bqhe', a.astype(v.dtype), v)

    o = lax.map(one_block, (q_blocks, jnp.arange(nb)))
    o = jnp.moveaxis(o, 0, 1).reshape(B, S, H, dv)
    o = _rms_norm(o, out_g) * (1.0 - lam_init)
    return o.reshape(B, S, H * dv)


def _conformer_conv(z, w, b, ng, nb_):
    a, g = jnp.split(z, 2, axis=-1)
    y = a * jax.nn.sigmoid(g)
    y = lax.conv_general_dilated(
        y, w[:, None, :].astype(y.dtype), window_strides=(1,),
        padding=[(CONV_KERNEL - 1, 0)],
        dimension_numbers=('NWC', 'WIO', 'NWC'),
        feature_group_count=CONV_W) + b
    return jax.nn.silu(_layer_norm(y, ng, nb_))


def _chunked_spatial_gating(z, ng, nb_, ws, bs):
    u, v = jnp.split(jax.nn.gelu(z), 2, axis=-1)
    v = _layer_norm(v, ng, nb_)
    B, S, _ = v.shape
    nc = S // CHUNK
    vh = v.reshape(B, nc, CHUNK, GMLP_HEADS, HEAD_DIM)
    w = ws * jnp.tril(jnp.ones((CHUNK, CHUNK), ws.dtype))[None]
    s = jnp.einsum('hts,bnshd->bnthd', w, vh) + jnp.transpose(bs)[None, None, :, :, None]
    return u * s.reshape(B, S, GMLP_W)


def _forgetting_attention(q, k, v, log_f):
    B, S, H, d = q.shape
    nb = S // Q_BLOCK
    cum = jnp.cumsum(log_f.astype(jnp.float32), axis=1)
    q_blocks = jnp.moveaxis(q.reshape(B, nb, Q_BLOCK, H, d), 1, 0)
    c_blocks = jnp.moveaxis(cum.reshape(B, nb, Q_BLOCK, H), 1, 0)
    cum_k = jnp.transpose(cum, (0, 2, 1))
    k_pos = jnp.arange(S)
    scale = d ** -0.5

    def one_block(args):
        qi, ci, i = args
        s = jnp.einsum('bqhd,bkhd->bhqk', qi, k).astype(jnp.float32) * scale
        s = s + jnp.transpose(ci, (0, 2, 1))[..., None] - cum_k[:, :, None, :]
        q_pos = i * Q_BLOCK + jnp.arange(Q_BLOCK)
        causal = k_pos[None, :] <= q_pos[:, None]
        p = jax.nn.softmax(jnp.where(causal, s, NEG_INF), axis=-1)
        return jnp.einsum('bhqk,bkhd->bqhd', p.astype(v.dtype), v)

    o = lax.map(one_block, (q_blocks, c_blocks, jnp.arange(nb)))
    return jnp.moveaxis(o, 0, 1).reshape(B, S, H * d)


def setup_inputs(seed: int = 0) -> dict:
    key = jax.random.key(seed)
    ks = jax.random.split(key, 24)
    f32 = jnp.float32

    def nrm(k, shape, s):
        return jax.random.normal(k, shape, f32) * s

    L = DEPTH
    return {
        'x': nrm(ks[0], (BATCH, SEQ, D_MODEL), 1.0),
        'c': nrm(ks[1], (BATCH, D_MODEL), 1.0),
        'ada_w': nrm(ks[2], (L, D_MODEL, N_MOD * D_MODEL), 0.5 * D_MODEL ** -0.5),
        'ada_b': nrm(ks[3], (L, N_MOD * D_MODEL), 0.02),
        'norm_g': 1.0 + nrm(ks[4], (L, 3, D_MODEL), 0.05),
        'ffn1_w_in': nrm(ks[5], (L, D_MODEL, 2 * D_FF), D_MODEL ** -0.5),
        'ffn1_w_out': nrm(ks[6], (L, D_FF, D_MODEL), D_FF ** -0.5),
        'ffn2_w_in': nrm(ks[7], (L, D_MODEL, 2 * D_FF), D_MODEL ** -0.5),
        'ffn2_w_out': nrm(ks[8], (L, D_FF, D_MODEL), D_FF ** -0.5),
        'w_in': nrm(ks[9], (L, D_MODEL, IN_COLS), D_MODEL ** -0.5),
        'w_out': nrm(ks[10], (L, MIX_W, D_MODEL), MIX_W ** -0.5),
        'diff_qk_g': 1.0 + nrm(ks[11], (L, 2, DIFF_QK_DIM), 0.05),
        'diff_lambda': nrm(ks[12], (L, 4, DIFF_QK_DIM), 0.1),
        'diff_out_g': 1.0 + nrm(ks[13], (L, HEAD_DIM), 0.05),
        'conv_w': nrm(ks[14], (L, CONV_KERNEL, CONV_W), CONV_KERNEL ** -0.5),
        'conv_b': nrm(ks[15], (L, CONV_W), 0.02),
        'conv_norm_g': 1.0 + nrm(ks[16], (L, CONV_W), 0.05),
        'conv_norm_b': nrm(ks[17], (L, CONV_W), 0.02),
        'gmlp_norm_g': 1.0 + nrm(ks[18], (L, GMLP_W), 0.05),
        'gmlp_norm_b': nrm(ks[19], (L, GMLP_W), 0.02),
        'gmlp_ws': nrm(ks[20], (L, GMLP_HEADS, CHUNK, CHUNK), CHUNK ** -0.5),
        'gmlp_bs': 1.0 + nrm(ks[21], (L, GMLP_HEADS, CHUNK), 0.1),
        'fox_qk_g': 1.0 + nrm(ks[22], (L, 2, HEAD_DIM), 0.05),
        'fox_fb': 2.0 + nrm(ks[23], (L, FOX_HEADS), 0.5),
    }


def reference(x, c, ada_w, ada_b, norm_g, ffn1_w_in, ffn1_w_out, ffn2_w_in, ffn2_w_out,
              w_in, w_out, diff_qk_g, diff_lambda, diff_out_g, conv_w, conv_b,
              conv_norm_g, conv_norm_b, gmlp_norm_g, gmlp_norm_b, gmlp_ws, gmlp_bs,
              fox_qk_g, fox_fb):
    B, S, _ = x.shape
    h = x
    cond = jax.nn.silu(c)
    for l in range(DEPTH):
        mod = cond @ ada_w[l] + ada_b[l]
        sh1, sc1, g1, sh2, sc2, g2, sh3, sc3, g3 = jnp.split(mod, N_MOD, axis=-1)

        hn = _modulate(_rms_norm(h, norm_g[l, 0]), sh1, sc1)
        h = h + 0.5 * g1[:, None, :] * _swiglu(hn, ffn1_w_in[l], ffn1_w_out[l])

        hn = _modulate(_rms_norm(h, norm_g[l, 1]), sh2, sc2)
        z = hn @ w_in[l]

        qa = _rms_norm(z[..., DIFF_Q0:DIFF_K0].reshape(B, S, DIFF_HEADS, 2, DIFF_QK_DIM), diff_qk_g[l, 0])
        ka = _rms_norm(z[..., DIFF_K0:DIFF_V0].reshape(B, S, DIFF_HEADS, 2, DIFF_QK_DIM), diff_qk_g[l, 1])
        va = z[..., DIFF_V0:CONV0].reshape(B, S, DIFF_HEADS, HEAD_DIM)
        lam_init = 0.8 - 0.6 * math.exp(-0.3 * l)
        lv = diff_lambda[l].astype(jnp.float32)
        lam = jnp.exp(jnp.sum(lv[0] * lv[1])) - jnp.exp(jnp.sum(lv[2] * lv[3])) + lam_init
        o_a = _differential_attention(qa, ka, va, lam, lam_init, diff_out_g[l])

        o_b = _conformer_conv(z[..., CONV0:GMLP0], conv_w[l], conv_b[l], conv_norm_g[l], conv_norm_b[l])

        o_c = _chunked_spatial_gating(z[..., GMLP0:FOX_Q0], gmlp_norm_g[l], gmlp_norm_b[l], gmlp_ws[l], gmlp_bs[l])

        qd = _rms_norm(z[..., FOX_Q0:FOX_K0].reshape(B, S, FOX_HEADS, HEAD_DIM), fox_qk_g[l, 0])
        kd = _rms_norm(z[..., FOX_K0:FOX_V0].reshape(B, S, FOX_HEADS, HEAD_DIM), fox_qk_g[l, 1])
        vd = z[..., FOX_V0:FOX_F0].reshape(B, S, FOX_HEADS, HEAD_DIM)
        log_f = jax.nn.log_sigmoid((z[..., FOX_F0:IN_COLS] + fox_fb[l]).astype(jnp.float32))
        o_d = _forgetting_attention(qd, kd, vd, log_f)

        mixed = jnp.concatenate([o_a, o_b, o_c, o_d], axis=-1)
        h = h + g2[:, None, :] * (mixed @ w_out[l])

        hn = _modulate(_rms_norm(h, norm_g[l, 2]), sh3, sc3)
        h = h + 0.5 * g3[:, None, :] * _swiglu(hn, ffn2_w_in[l], ffn2_w_out[l])
    return h
```

```python
import contextlib
import math
import numpy as np
import ml_dtypes
import concourse.bass as bass
import concourse.mybir as mybir
from concourse.bass_utils import run_bass_kernel_spmd

F32 = mybir.dt.float32
BF16 = mybir.dt.bfloat16
AF = mybir.ActivationFunctionType
ALU = mybir.AluOpType
AX = mybir.AxisListType

D = 1024
T = 2048
NT = 4
TS = 512
FF = 2816
NFC = 22
EPS = 1e-6
IN_COLS = 2564
SEM_LIMIT = 12000
SB_BASE = 16640
SB_END = 229344
SLOT_BYTES = 24832
NEG = -30000.0


class _Op:
    __slots__ = ("eng", "fn", "reads", "writes", "dma", "chan", "idx", "deps",
                 "marked", "sem", "val", "n_dma", "waits", "custom_inc")

    def __init__(self, eng, fn, reads, writes, dma=False, chan=None, n_dma=0):
        self.eng = eng
        self.fn = fn
        self.reads = tuple(reads)
        self.writes = tuple(writes)
        self.dma = dma
        self.chan = chan
        self.n_dma = n_dma
        self.deps = []
        self.marked = False
        self.sem = None
        self.val = 0
        self.waits = []
        self.custom_inc = None


class Prog:
    ENGS = ("pe", "act", "dve", "pool", "sp")

    def __init__(self, nc):
        self.nc = nc
        self.ops = []
        self.last_writer = {}
        self.readers = {}
        self.chan_last = {}
        self.last_on_eng = {}
        self.pending_barrier = {}

    def barrier(self):
        deps = list(self.last_on_eng.values()) + list(self.chan_last.values())
        for e in self.ENGS:
            self.pending_barrier[e] = list(deps)

    def _add(self, op, nobarrier=False):
        op.idx = len(self.ops)
        deps = set()
        for r in op.reads:
            w = self.last_writer.get(r)
            if w is not None:
                deps.add((w, "raw"))
        for wkey in op.writes:
            w = self.last_writer.get(wkey)
            if w is not None:
                deps.add((w, "waw"))
            lastrd = {}
            for rd in self.readers.get(wkey, ()):
                if rd is op:
                    continue
                if rd.dma:
                    deps.add((rd, "war"))
                else:
                    lastrd[rd.eng] = rd
            for rd in lastrd.values():
                deps.add((rd, "war"))
        if op.dma:
            prev = self.chan_last.get(op.chan)
            if prev is not None:
                deps.add((prev, "raw"))
            self.chan_last[op.chan] = op
        if not nobarrier and self.pending_barrier.get(op.eng):
            for d in self.pending_barrier[op.eng]:
                deps.add((d, "raw"))
            self.pending_barrier[op.eng] = None
        best = {}
        for d, kind in deps:
            if d is op:
                continue
            if (not d.dma) and (not op.dma) and d.eng == op.eng:
                if op.eng == "pe":
                    continue
                if kind != "raw":
                    continue
            best[d.idx] = d
        op.deps = list(best.values())
        for d in op.deps:
            d.marked = True
        for r in op.reads:
            self.readers.setdefault(r, []).append(op)
        for wkey in op.writes:
            self.last_writer[wkey] = op
            self.readers[wkey] = []
        if not op.dma:
            self.last_on_eng[op.eng] = op
        self.ops.append(op)
        return op

    def op(self, eng, fn, reads=(), writes=()):
        return self._add(_Op(eng, fn, reads, writes))

    def dma(self, eng, chan, transfers, reads=(), writes=(), nobarrier=False):
        return self._add(_Op(eng, transfers, reads, writes, dma=True, chan=chan,
                             n_dma=len(transfers)), nobarrier=nobarrier)

    def custom(self, eng, chan, fn, inc, reads=(), writes=()):
        op = _Op(eng, fn, reads, writes, dma=True, chan=chan, n_dma=0)
        op.custom_inc = inc
        return self._add(op)

    def emit(self):
        nc = self.nc
        eng_count = {e: 0 for e in self.ENGS}
        chan_count = {}
        sem_names = set()
        for op in self.ops:
            if op.dma:
                c = chan_count.get(op.chan, 0) + (op.custom_inc if op.custom_inc else 16 * op.n_dma)
                chan_count[op.chan] = c
                op.sem = "c_" + op.chan
                op.val = c
                sem_names.add(op.sem)
            elif op.marked:
                k = eng_count[op.eng]
                eng_count[op.eng] = k + 1
                op.sem = "e_%s_%d" % (op.eng, k // SEM_LIMIT)
                op.val = (k % SEM_LIMIT) + 1
                sem_names.add(op.sem)
        seen = {e: {} for e in self.ENGS}
        n_waits = 0
        for op in self.ops:
            need = {}
            for d in op.deps:
                if d.val > need.get(d.sem, 0):
                    need[d.sem] = d.val
            sd = seen[op.eng]
            ws = []
            for s, v in need.items():
                if sd.get(s, 0) >= v:
                    continue
                sd[s] = v
                ws.append((s, v))
            op.waits = ws
            n_waits += len(ws)
        self.stats = dict(n_ops=len(self.ops), n_waits=n_waits, eng_count=dict(eng_count),
                          chan_count=dict(chan_count), n_sems=len(sem_names),
                          per_eng={e: sum(1 for o in self.ops if o.eng == e) for e in self.ENGS})
        sem_names = sorted(sem_names)
        with contextlib.ExitStack() as st:
            sems = {n: st.enter_context(nc.semaphore(n)) for n in sem_names}
            block = st.enter_context(nc.Block())
            by_eng = {e: [o for o in self.ops if o.eng == e] for e in self.ENGS}
            final_waits = [("c_" + c, v) for c, v in chan_count.items()]

            def run(engobj, ename):
                for op in by_eng[ename]:
                    for s, v in op.waits:
                        engobj.wait_ge(sems[s], v)
                    if op.custom_inc:
                        op.fn(engobj).then_inc(sems[op.sem], op.custom_inc)
                    elif op.dma:
                        for (o, i) in op.fn:
                            engobj.dma_start(out=o, in_=i).then_inc(sems[op.sem], 16)
                    else:
                        ins = op.fn(engobj)
                        if op.marked:
                            ins.then_inc(sems[op.sem], 1)
                if ename == "sp":
                    for s, v in final_waits:
                        engobj.wait_ge(sems[s], v)

            @block.tensor
            def _(e):
                run(e, "pe")

            @block.scalar
            def _(e):
                run(e, "act")

            @block.vector
            def _(e):
                run(e, "dve")

            @block.gpsimd
            def _(e):
                run(e, "pool")

            @block.sync
            def _(e):
                run(e, "sp")


C_IDENT = 0
C_TRIU = 128
C_ONES = 256
NCF = 384
B_IDENT = 0
B_ONES = 128
B_BLK32 = 256
B_BLK64 = 384
B_SEL96 = 512
B_MASK = 640
B_TRIU = 640 + 2048
NCB = B_TRIU + 128


def _make_consts():
    cf = np.zeros((128, NCF), np.float32)
    cf[:, C_IDENT:C_IDENT + 128] = np.eye(128, dtype=np.float32)
    s = np.arange(128)[:, None]
    t = np.arange(128)[None, :]
    cf[:, C_TRIU:C_TRIU + 128] = (s <= t).astype(np.float32)
    cf[:, C_ONES:C_ONES + 128] = 1.0
    cb = np.zeros((128, NCB), np.float32)
    cb[:, B_IDENT:B_IDENT + 128] = np.eye(128, dtype=np.float32)
    cb[:, B_ONES:B_ONES + 128] = 1.0
    cb[:, B_BLK32:B_BLK32 + 128] = (s // 32 == t // 32).astype(np.float32)
    cb[:, B_BLK64:B_BLK64 + 128] = (s // 64 == t // 64).astype(np.float32)
    sel = np.zeros((128, 128), np.float32)
    sel[[0, 32, 64], :] = 1.0
    cb[:, B_SEL96:B_SEL96 + 128] = sel
    q = np.arange(512)[None, :]
    for a in range(4):
        allowed = (a * 128 + s) <= q
        cb[:, B_MASK + a * 512:B_MASK + (a + 1) * 512] = np.where(allowed, 0.0, NEG)
    cb[:, B_TRIU:B_TRIU + 128] = (s <= t).astype(np.float32)
    return cf, cb


class Builder:
    def __init__(self, stages, layers_in_launch, dbg=None, fused=False, cc_inc=1, cc_pairs=True):
        self.stages = stages
        self.fused = fused
        self.cc_inc = cc_inc
        self.cc_pairs = cc_pairs
        self.dbg = dbg or {}
        nc = bass.Bass("TRN2", target_bir_lowering=False)
        self.nc = nc
        self.P = Prog(nc)
        self.off = SB_BASE
        self.uid = 0
        self._declare_dram(layers_in_launch)
        self._alloc_persistent()

    def sb(self, name, shape, dt):
        nbytes = int(np.prod(shape[1:])) * (4 if dt == F32 else 2)
        nbytes = (nbytes + 63) // 64 * 64
        assert self.off + nbytes <= SB_END, (name, self.off, nbytes)
        self.uid += 1
        h = self.nc.alloc_sbuf_tensor_at("%s_%d" % (name, self.uid), list(shape), dt, offset=self.off)
        self.off += nbytes
        return h

    def _declare_dram(self, layers):
        nc = self.nc
        di = lambda n, s, dt=F32: nc.dram_tensor(n, list(s), dt, kind="ExternalInput").ap()
        do = lambda n, s, dt=F32: nc.dram_tensor(n, list(s), dt, kind="ExternalOutput").ap()
        self.d = {}
        d = self.d
        d["hin"] = di("hin", [D, T])
        d["hout"] = do("hout", [D, T])
        d["cT"] = di("cT", [128, 8])
        d["consts_f"] = di("consts_f", [128, NCF])
        d["consts_b"] = di("consts_b", [128, NCB])
        d["ada_bT"] = di("ada_bT", [2, 128, 72])
        d["norm_gT"] = di("norm_gT", [128, 48])
        d["qkg"] = di("qkg", [128, 2, 4])
        d["dlam"] = di("dlam", [128, 2, 4, 32])
        d["doutg"] = di("doutg", [128, 2, 64])
        d["convw"] = di("convw", [128, 2, 2, 31])
        d["convp"] = di("convp", [128, 2, 3, 2])
        d["gmlpn"] = di("gmlpn", [128, 2, 2, 256])
        d["gmlpwT"] = di("gmlpwT", [2, 128, 4, 128])
        d["gmlpbs"] = di("gmlpbs", [2, 128, 2, 128])
        d["foxfb"] = di("foxfb", [128, 2, 4])
        self.layers = sorted(layers)
        for l in self.layers:
            d["ada_w%d" % l] = di("ada_w%d" % l, [D, 9 * D])
            d["f1wi%d" % l] = di("f1wi%d" % l, [D, 2 * FF])
            d["f1wo%d" % l] = di("f1wo%d" % l, [FF, D])
            d["f2wi%d" % l] = di("f2wi%d" % l, [D, 2 * FF])
            d["f2wo%d" % l] = di("f2wo%d" % l, [FF, D])
            d["w_in%d" % l] = di("w_in%d" % l, [D, IN_COLS])
            d["w_out%d" % l] = di("w_out%d" % l, [D, D])
        kinds = {}
        for st in self.stages:
            if st[0] == "kv":
                kinds["kv_out"] = st[1]
            if st[0] == "mix":
                kinds["kv_in"] = st[1]
        if self.fused:
            nc_ = self.nc
            G = 2 if self.cc_pairs else 8
            self.G = G
            d["flag"] = di("flag", [128, 1])
            for l in self.layers:
                d["bK%d" % l] = nc_.dram_tensor("bK%d" % l, [512, T], BF16).ap()
                d["gK%d" % l] = nc_.dram_tensor("gK%d" % l, [G * 512, T], BF16).ap()
                d["bV%d" % l] = nc_.dram_tensor("bV%d" % l, [8 * 16 * 128, 65], BF16).ap()
                d["gV%d" % l] = nc_.dram_tensor("gV%d" % l, [G * 8 * 16 * 128, 65], BF16).ap()
                d["bF%d" % l] = nc_.dram_tensor("bF%d" % l, [128, 124], F32).ap()
                d["gF%d" % l] = nc_.dram_tensor("gF%d" % l, [G * 128, 124], F32).ap()
        elif "kv_out" in kinds:
            d["kT_o"] = do("kT_o", [8, 64, T], BF16)
            d["vT_o"] = do("vT_o", [8, 16, 128, 65], BF16)
            d["cum_o"] = do("cum_o", [128, 16, 4])
            d["halo_o"] = do("halo_o", [128, 2, 30])
        if "kv_in" in kinds and not self.fused:
            d["kT_i"] = di("kT_i", [8, 64, 2 * T], BF16)
            d["vT_i"] = di("vT_i", [8, 32, 128, 65], BF16)
            d["cum_i"] = di("cum_i", [2, 128, 16, 4])
            d["halo_i"] = di("halo_i", [128, 2, 30])
        for name, shape in self.dbg.items():
            d["dbg_" + name] = do("dbg_" + name, shape)

    def _alloc_persistent(self):
        nc = self.nc
        self.hT = self.sb("hT", [128, 8, T], F32)
        self.cf = self.sb("cf", [128, NCF], F32)
        self.cb = self.sb("cb", [128, NCB], BF16)
        self.slot = [self.sb("slot%d" % i, [128, SLOT_BYTES // 2], BF16) for i in range(2)]
        self.slot_n = 0
        self.modT = self.sb("modT", [128, 72], F32)
        self.ada_b = self.sb("ada_b", [128, 2, 72], F32)
        self.norm_g = self.sb("norm_g", [128, 48], F32)
        self.mods = self.sb("mods", [128, 9, 8], F32)
        self.cond = self.sb("cond", [128, 8], BF16)
        self.cTf = self.sb("cTf", [128, 8], F32)
        self.small = {}
        for n, shp in (("qkg", [128, 2, 4]), ("dlam", [128, 2, 4, 32]), ("doutg", [128, 2, 64]),
                       ("convw", [128, 2, 2, 31]), ("convp", [128, 2, 3, 2]), ("gmlpn", [128, 2, 2, 256]),
                       ("foxfb", [128, 2, 4])):
            self.small[n] = self.sb(n, shp, F32)
        self.ov_base = self.off
        self.ps = [nc.alloc_psum_tensor("ps%d" % i, [128, 512], F32) for i in range(7)]
        self.ps_bf = nc.alloc_psum_tensor("psbf", [128, 1024], BF16)
        self.misc_rr = 0

    def x_kst(self, l, h8, tt):
        if self.fused:
            return self.d["bK%d" % l][h8 * 64:(h8 + 1) * 64, tt * TS:(tt + 1) * TS]
        return self.d["kT_o"][h8, :, tt * TS:(tt + 1) * TS]

    def x_vst(self, l, slot):
        if self.fused:
            v = self.d["bV%d" % l].rearrange("(h s p) e -> h s p e", h=8, s=16)
            return v[:, slot, :, :].rearrange("h p e -> p h e")
        return self.d["vT_o"][:, slot, :, :].rearrange("h p e -> p h e")

    def x_cumst(self, l):
        if self.fused:
            return self.d["bF%d" % l][:, 0:64].rearrange("p (s h) -> p s h", s=16)
        return self.d["cum_o"]

    def x_halost(self, l):
        if self.fused:
            return self.d["bF%d" % l][:, 64:124].rearrange("p (c j) -> p c j", c=2)
        return self.d["halo_o"]

    def x_kld(self, l, h8, ci, ns):
        if self.fused:
            src = self.d["gK%d" % l] if ci < 2 else self.d["bK%d" % l]
            c0 = (ci % 2) * 1024
            return src[h8 * 64:(h8 + 1) * 64, c0:c0 + ns * 128]
        return self.d["kT_i"][h8, :, ci * 1024:ci * 1024 + ns * 128]

    def x_vld(self, l, h8, ci, ns):
        if self.fused:
            src = self.d["gV%d" % l][0:8 * 16 * 128, :] if ci < 2 else self.d["bV%d" % l]
            v = src.rearrange("(h s p) e -> h s p e", h=8, s=16)
            s0 = (ci % 2) * 8
            return v[h8, s0:s0 + ns, :, :].rearrange("s p e -> p s e")
        return self.d["vT_i"][h8, ci * 8:ci * 8 + ns, :, :].rearrange("s p e -> p s e")

    def x_reads(self, l, ci):
        if self.fused:
            return ["gat%d" % l] if ci < 2 else ["blk_kst%d" % l, "blk_vst%d" % l]
        return []

    def exchange(self, l):
        P, d = self.P, self.d
        groups = [[0, 1], [2, 3], [4, 5], [6, 7]] if self.cc_pairs else [list(range(8))]
        rd = ["blk_kst%d" % l, "blk_vst%d" % l, "blk_cum%d" % l, "blk_halo%d" % l]
        for nm in ("K", "V", "F"):
            bi, go = d["b%s%d" % (nm, l)], d["g%s%d" % (nm, l)]
            P.custom("pool", "cc%s%d" % (nm, l),
                     lambda e, bi=bi, go=go: e.collective_compute(
                         "AllGather", ALU.bypass, replica_groups=groups, ins=[bi.opt()], outs=[go.opt()]),
                     self.cc_inc, reads=rd, writes=["gat%d" % l])

    def overlay_reset(self):
        self.off = self.ov_base
        self.P.barrier()

    def wload(self, transfers_fn, tag):
        i = self.slot_n % 2
        self.slot_n += 1
        s = self.slot[i]
        key = "slot%d" % i
        self.P.dma("pool", "w%d" % i, transfers_fn(s), writes=[key], nobarrier=True)
        return s, key

    def setup(self):
        P, d = self.P, self.d
        P.dma("sp", "ld", [(self.hT[:, c, :], d["hin"][c * 128:(c + 1) * 128, :]) for c in range(8)],
              writes=["hT%d" % t for t in range(NT)])
        P.dma("sp", "ld2", [(self.cf[:, :], d["consts_f"]), (self.cTf[:, :], d["cT"]),
                            (self.ada_b[:, :, :], d["ada_bT"].rearrange("l p j -> p l j")),
                            (self.norm_g[:, :], d["norm_gT"])]
              + [(self.small[n][:], d[n]) for n in self.small],
              writes=["cf", "cTf", "ada_b", "norm_g", "small"])
        P.dma("pool", "ldc", [(self.cb[:, :], d["consts_b"])], writes=["cb"])
        P.op("act", lambda e: e.activation(out=self.cond[:, :], in_=self.cTf[:, :], func=AF.Silu),
             reads=["cTf"], writes=["cond"])

    def finish(self):
        P, d = self.P, self.d
        P.dma("sp", "st", [(d["hout"][c * 128:(c + 1) * 128, :], self.hT[:, c, :]) for c in range(8)],
              reads=["hT%d" % t for t in range(NT)])

    def dump(self, name, src_ap, reads):
        self.P.dma("sp", "dbg", [(self.d["dbg_" + name], src_ap)], reads=reads)

    def ada(self, l):
        P, d = self.P, self.d
        aw = d["ada_w%d" % l].rearrange("(kc p) n -> p kc n", p=128)
        ps = self.ps[6]
        GC = 1536
        for g in range(6):
            def tf(s, g=g):
                v = s[:, 0:8 * GC].rearrange("p (kc n) -> p kc n", kc=8)
                return [(v[:, kc, :], aw[:, kc, g * GC:(g + 1) * GC]) for kc in range(8)]
            s, key = self.wload(tf, "ada")
            v = s[:, 0:8 * GC].rearrange("p (kc n) -> p kc n", kc=8)
            for jj in range(12):
                j = g * 12 + jj
                for kc in range(8):
                    P.op("pe", lambda e, v=v, jj=jj, kc=kc, j=j: e.matmul(
                        ps[:, j:j + 1], lhsT=v[:, kc, jj * 128:(jj + 1) * 128], rhs=self.cond[:, kc:kc + 1],
                        start=(kc == 0), stop=(kc == 7)),
                        reads=[key, "cond"], writes=["ps6"])
        P.op("dve", lambda e: e.tensor_tensor(out=self.modT[:, :], in0=ps[:, 0:72], in1=self.ada_b[:, l, :],
                                              op=ALU.add),
             reads=["ps6", "ada_b"], writes=["modT"])
        for i in range(3):
            sh = self.modT[:, (3 * i) * 8:(3 * i) * 8 + 8]
            sc = self.modT[:, (3 * i + 1) * 8:(3 * i + 1) * 8 + 8]
            g = self.modT[:, (3 * i + 2) * 8:(3 * i + 2) * 8 + 8]
            ng = self.norm_g[:, (l * 3 + i) * 8:(l * 3 + i) * 8 + 8]
            P.op("dve", lambda e, sc=sc, ng=ng, i=i: e.scalar_tensor_tensor(
                out=self.mods[:, 3 * i, :], in0=sc, scalar=1.0, in1=ng, op0=ALU.add, op1=ALU.mult),
                reads=["modT", "norm_g"], writes=["mods"])
            P.op("dve", lambda e, sh=sh, i=i: e.tensor_copy(out=self.mods[:, 3 * i + 1, :], in_=sh),
                 reads=["modT"], writes=["mods"])
            gs = 1.0 if i == 1 else 0.5
            P.op("dve", lambda e, g=g, i=i, gs=gs: e.tensor_scalar(
                out=self.mods[:, 3 * i + 2, :], in0=g, scalar1=gs, scalar2=None, op0=ALU.mult),
                reads=["modT"], writes=["mods"])

    def hn_tile(self, i, tt, hn_out, hn_key, sq, tmp2, rstd):
        P = self.P
        ps = self.ps[6]
        hT_t = self.hT[:, :, tt * TS:(tt + 1) * TS]
        P.op("act", lambda e: e.activation(out=sq[:, :, :], in_=hT_t, func=AF.Square),
             reads=["hT%d" % tt], writes=["sq"])
        for kc in range(8):
            P.op("pe", lambda e, kc=kc: e.matmul(ps[:, :], lhsT=self.cb[:, B_ONES:B_ONES + 128], rhs=sq[:, kc, :],
                                                 start=(kc == 0), stop=(kc == 7)),
                 reads=["sq", "cb"], writes=["ps6"])
        P.op("act", lambda e: e.activation(out=rstd[:, :], in_=ps[:, :], func=AF.Sqrt, bias=EPS, scale=1.0 / D),
             reads=["ps6"], writes=["rstd"])
        P.op("dve", lambda e: e.reciprocal(out=rstd[:, :], in_=rstd[:, :]), reads=["rstd"], writes=["rstd"])
        for kc in range(8):
            tb = tmp2[kc % 2]
            tk = "tmp2_%d" % (kc % 2)
            P.op("dve", lambda e, kc=kc, tb=tb: e.scalar_tensor_tensor(
                out=tb[:, :], in0=self.hT[:, kc, tt * TS:(tt + 1) * TS], scalar=self.mods[:, 3 * i, kc:kc + 1],
                in1=rstd[:, :], op0=ALU.mult, op1=ALU.mult),
                reads=["hT%d" % tt, "mods", "rstd"], writes=[tk])
            P.op("act", lambda e, kc=kc, tb=tb: e.activation(
                out=hn_out[:, kc, :], in_=tb[:, :], func=AF.Identity, bias=self.mods[:, 3 * i + 1, kc:kc + 1],
                scale=1.0),
                reads=[tk, "mods"], writes=[hn_key])

    def ffn(self, l, which):
        P, d = self.P, self.d
        self.overlay_reset()
        i = 0 if which == 1 else 2
        wi = d["f%dwi%d" % (which, l)].rearrange("(kc p) n -> p kc n", p=128)
        wo = d["f%dwo%d" % (which, l)].rearrange("(fc p) n -> p fc n", p=128)
        hnT = self.sb("hnT", [128, 8, T], BF16)
        sq = self.sb("sq", [128, 8, TS], BF16)
        tmp2 = [self.sb("tmp2a", [128, TS], F32), self.sb("tmp2b", [128, TS], F32)]
        rstd = self.sb("rstd", [128, TS], F32)
        sg = [self.sb("sg%d" % k, [128, TS], F32) for k in range(2)]
        act = [self.sb("act%d" % k, [128, 4, TS], BF16) for k in range(2)]
        for tt in range(NT):
            self.hn_tile(i, tt, hnT[:, :, tt * TS:(tt + 1) * TS], "hnT%d" % tt, sq, tmp2, rstd)
        portions = [(0, 4), (4, 4), (8, 4), (12, 4), (16, 3), (19, 3)]
        cnt = 0
        ocnt = 0
        for (f0, nf) in portions:
            def tf(s, f0=f0, nf=nf):
                vi = s[:, 0:8 * 1024].rearrange("p (kc n) -> p kc n", kc=8)
                vo = s[:, 8192:8192 + 4 * 1024].rearrange("p (f n) -> p f n", f=4)
                tr = []
                for kc in range(8):
                    tr.append((vi[:, kc, 0:nf * 128], wi[:, kc, f0 * 128:(f0 + nf) * 128]))
                    tr.append((vi[:, kc, 512:512 + nf * 128], wi[:, kc, FF + f0 * 128:FF + (f0 + nf) * 128]))
                tr.append((vo[:, 0:nf, :], wo[:, f0:f0 + nf, :]))
                return tr
            s, key = self.wload(tf, "ffn")
            vi = s[:, 0:8 * 1024].rearrange("p (kc n) -> p kc n", kc=8)
            vo = s[:, 8192:8192 + 4 * 1024].rearrange("p (f n) -> p f n", f=4)
            for tt in range(NT):
                ab = act[tt % 2]
                ak = "act%d" % (tt % 2)
                hn_t = hnT[:, :, tt * TS:(tt + 1) * TS]
                for fl in range(nf):
                    gb, ub = self.ps[cnt % 2], self.ps[2 + cnt % 2]
                    gk, uk = "ps%d" % (cnt % 2), "ps%d" % (2 + cnt % 2)
                    sgb, sgk = sg[cnt % 2], "sg%d" % (cnt % 2)
                    cnt += 1
                    for kc in range(8):
                        P.op("pe", lambda e, kc=kc, fl=fl, gb=gb, hn_t=hn_t, vi=vi: e.matmul(
                            gb[:, :], lhsT=vi[:, kc, fl * 128:(fl + 1) * 128], rhs=hn_t[:, kc, :],
                            start=(kc == 0), stop=(kc == 7)),
                            reads=[key, "hnT%d" % tt], writes=[gk])
                    for kc in range(8):
                        P.op("pe", lambda e, kc=kc, fl=fl, ub=ub, hn_t=hn_t, vi=vi: e.matmul(
                            ub[:, :], lhsT=vi[:, kc, 512 + fl * 128:512 + (fl + 1) * 128], rhs=hn_t[:, kc, :],
                            start=(kc == 0), stop=(kc == 7)),
                            reads=[key, "hnT%d" % tt], writes=[uk])
                    P.op("act", lambda e, gb=gb, sgb=sgb: e.activation(out=sgb[:, :], in_=gb[:, :], func=AF.Silu),
                         reads=[gk], writes=[sgk])
                    P.op("dve", lambda e, ub=ub, sgb=sgb, ab=ab, fl=fl: e.tensor_tensor(
                        out=ab[:, fl, :], in0=sgb[:, :], in1=ub[:, :], op=ALU.mult),
                        reads=[sgk, uk], writes=[ak])
                for m in range(8):
                    ob, ok = self.ps[4 + ocnt % 2], "ps%d" % (4 + ocnt % 2)
                    ocnt += 1
                    for fl in range(nf):
                        P.op("pe", lambda e, fl=fl, m=m, ob=ob, ab=ab, vo=vo: e.matmul(
                            ob[:, :], lhsT=vo[:, fl, m * 128:(m + 1) * 128], rhs=ab[:, fl, :],
                            start=(fl == 0), stop=(fl == nf - 1)),
                            reads=[key, ak], writes=[ok])
                    P.op("dve", lambda e, m=m, ob=ob, tt=tt: e.scalar_tensor_tensor(
                        out=self.hT[:, m, tt * TS:(tt + 1) * TS], in0=ob[:, :],
                        scalar=self.mods[:, 3 * i + 2, m:m + 1], in1=self.hT[:, m, tt * TS:(tt + 1) * TS],
                        op0=ALU.mult, op1=ALU.add),
                        reads=[ok, "mods", "hT%d" % tt], writes=["hT%d" % tt])

    def qk_head(self, wv, key, c0, hn_t, hnkey, blk, inv_d, gain_ap, out_ap, out_key, sqb, stdb):
        P = self.P
        pb = 4 + self.misc_rr % 2
        self.misc_rr += 1
        ps, pk = self.ps[pb], "ps%d" % pb
        for kc in range(8):
            P.op("pe", lambda e, kc=kc: e.matmul(ps[0:64, :], lhsT=wv[:, kc, c0:c0 + 64], rhs=hn_t[:, kc, :],
                                                 start=(kc == 0), stop=(kc == 7)),
                 reads=[key, hnkey], writes=[pk])
        P.op("act", lambda e: e.activation(out=sqb[0:64, :], in_=ps[0:64, :], func=AF.Square),
             reads=[pk], writes=["qsq"])
        ps2, pk2 = self.ps[6], "ps6"
        P.op("pe", lambda e: e.matmul(ps2[0:64, :], lhsT=self.cb[0:64, blk:blk + 64], rhs=sqb[0:64, :],
                                      start=True, stop=True),
             reads=["qsq", "cb"], writes=[pk2])
        P.op("act", lambda e: e.activation(out=stdb[0:64, :], in_=ps2[0:64, :], func=AF.Sqrt, bias=EPS,
                                           scale=inv_d),
             reads=[pk2], writes=["qstd"])
        P.op("dve", lambda e: e.reciprocal(out=stdb[0:64, :], in_=stdb[0:64, :]), reads=["qstd"], writes=["qstd"])
        P.op("dve", lambda e: e.scalar_tensor_tensor(out=out_ap, in0=ps[0:64, :], scalar=gain_ap,
                                                     in1=stdb[0:64, :], op0=ALU.mult, op1=ALU.mult),
             reads=[pk, "qstd", "gains"], writes=[out_key])

    def gains(self, l):
        P = self.P
        g = self.sb("gains", [128, 4], F32)
        qk = self.small["qkg"]
        for j, sc in enumerate((32.0 ** -0.5, 1.0, 64.0 ** -0.5, 1.0)):
            P.op("dve", lambda e, j=j, sc=sc: e.tensor_scalar(out=g[:, j:j + 1], in0=qk[:, l, j:j + 1], scalar1=sc,
                                                              scalar2=None, op0=ALU.mult),
                 reads=["small"], writes=["gains"])
        return g

    def kv(self, l):
        P, d = self.P, self.d
        self.overlay_reset()
        w = d["w_in%d" % l].rearrange("(kc p) n -> p kc n", p=128)
        NCK = 1540

        def tf(s):
            v = s[:, 0:8 * NCK].rearrange("p (kc n) -> p kc n", kc=8)
            tr = []
            for kc in range(8):
                tr.append((v[:, kc, 0:512], w[:, kc, 256:768]))
                tr.append((v[:, kc, 512:1028], w[:, kc, 2048:2564]))
                tr.append((v[:, kc, 1028:1540], w[:, kc, 768:1280]))
            return tr
        s, key = self.wload(tf, "kv")
        wv = s[:, 0:8 * NCK].rearrange("p (kc n) -> p kc n", kc=8)
        gains = self.gains(l)
        hn = self.sb("hn", [128, 8, TS], BF16)
        sq = self.sb("sq", [128, 8, TS], BF16)
        tmp2 = [self.sb("tmp2a", [128, TS], F32), self.sb("tmp2b", [128, TS], F32)]
        rstd = self.sb("rstd", [128, TS], F32)
        sqb = self.sb("qsq", [128, TS], BF16)
        stdb = self.sb("qstd", [128, TS], F32)
        kst = [self.sb("kst%d" % k, [128, TS], BF16) for k in range(2)]
        vst = [self.sb("vst%d" % k, [128, 8, 65], BF16) for k in range(2)]
        zf = self.sb("zf", [128, 16, 4], F32)
        nl = self.sb("nl", [128, 16, 4], F32)
        pre = self.sb("pre", [128, 16, 4], F32)
        cumN = self.sb("cumN", [128, 16, 4], F32)
        ysb = self.sb("ysb", [128, 2, TS], F32)
        sgb = self.sb("sgb", [128, TS], F32)
        for k in range(2):
            P.op("pool", lambda e, k=k: e.memset(vst[k][:, :, :], 1.0), writes=["vst%d" % k])
        kcnt = 0
        vcnt = 0
        for tt in range(NT):
            self.hn_tile(1, tt, hn[:, :, :], "hn", sq, tmp2, rstd)
            for h8 in range(8):
                grp = h8 // 4
                c0 = (0 if grp == 0 else 512) + (h8 % 4) * 64
                kb, kk = kst[kcnt % 2], "kst%d" % (kcnt % 2)
                kcnt += 1
                self.qk_head(wv, key, c0, hn, "hn", B_BLK32 if grp == 0 else B_BLK64,
                             1.0 / 32 if grp == 0 else 1.0 / 64,
                             gains[0:64, 1:2] if grp == 0 else gains[0:64, 3:4],
                             kb[0:64, :], kk, sqb, stdb)
                P.dma("sp", "kst", [(self.x_kst(l, h8, tt), kb[0:64, :])], reads=[kk], writes=["blk_kst%d" % l])
            for sub in range(4):
                slot = tt * 4 + sub
                vb, vk = vst[vcnt % 2], "vst%d" % (vcnt % 2)
                vcnt += 1
                for grp in range(2):
                    pb = 4 + self.misc_rr % 2
                    self.misc_rr += 1
                    ps, pk = self.ps[pb], "ps%d" % pb
                    c0 = 256 if grp == 0 else 768
                    for kc in range(8):
                        P.op("pe", lambda e, kc=kc, ps=ps, c0=c0, sub=sub: e.matmul(
                            ps[:, 0:256], lhsT=hn[:, kc, sub * 128:(sub + 1) * 128], rhs=wv[:, kc, c0:c0 + 256],
                            start=(kc == 0), stop=(kc == 7)),
                            reads=[key, "hn"], writes=[pk])
                    P.op("act", lambda e, ps=ps, vb=vb, grp=grp: e.activation(
                        out=vb[:, grp * 4:(grp + 1) * 4, 0:64],
                        in_=ps[:, 0:256].rearrange("p (h e) -> p h e", h=4), func=AF.Copy),
                        reads=[pk], writes=[vk])
                P.dma("sp", "vst", [(self.x_vst(l, slot), vb[:, :, :])], reads=[vk], writes=["blk_vst%d" % l])
                ps, pk = self.ps[6], "ps6"
                for kc in range(8):
                    P.op("pe", lambda e, kc=kc, sub=sub: e.matmul(
                        ps[:, 0:4], lhsT=hn[:, kc, sub * 128:(sub + 1) * 128], rhs=wv[:, kc, 1024:1028],
                        start=(kc == 0), stop=(kc == 7)),
                        reads=[key, "hn"], writes=[pk])
                P.op("dve", lambda e, slot=slot: e.tensor_tensor(out=zf[:, slot, :], in0=ps[:, 0:4],
                                                                 in1=self.small["foxfb"][:, l, :], op=ALU.add),
                     reads=[pk, "small"], writes=["zf"])
            if tt == NT - 1:
                for c in range(2):
                    pa, pka = self.ps[4], "ps4"
                    pg, pkg = self.ps[5], "ps5"
                    for kc in range(8):
                        P.op("pe", lambda e, kc=kc, c=c: e.matmul(
                            pa[:, :], lhsT=wv[:, kc, 1028 + c * 128:1028 + (c + 1) * 128], rhs=hn[:, kc, :],
                            start=(kc == 0), stop=(kc == 7)), reads=[key, "hn"], writes=[pka])
                    for kc in range(8):
                        P.op("pe", lambda e, kc=kc, c=c: e.matmul(
                            pg[:, :], lhsT=wv[:, kc, 1028 + 256 + c * 128:1028 + 256 + (c + 1) * 128],
                            rhs=hn[:, kc, :], start=(kc == 0), stop=(kc == 7)), reads=[key, "hn"], writes=[pkg])
                    P.op("act", lambda e: e.activation(out=sgb[:, :], in_=pg[:, :], func=AF.Sigmoid),
                         reads=[pkg], writes=["sgb"])
                    P.op("dve", lambda e, c=c: e.tensor_tensor(out=ysb[:, c, :], in0=sgb[:, :], in1=pa[:, :],
                                                               op=ALU.mult),
                         reads=["sgb", pka], writes=["ysb"])
                P.dma("sp", "halo", [(self.x_halost(l), ysb[:, :, TS - 30:TS])], reads=["ysb"], writes=["blk_halo%d" % l])
        P.op("act", lambda e: e.activation(out=nl[:, :, :], in_=zf[:, :, :], func=AF.Exp, scale=-1.0),
             reads=["zf"], writes=["nl"])
        P.op("act", lambda e: e.activation(out=nl[:, :, :], in_=nl[:, :, :], func=AF.Ln, bias=1.0),
             reads=["nl"], writes=["nl"])
        psc, pkc = self.ps[4], "ps4"
        pst, pkt = self.ps[5], "ps5"
        nl2 = nl[:, :, :].rearrange("p a b -> p (a b)")
        P.op("pe", lambda e: e.matmul(psc[:, 0:64], lhsT=self.cf[:, C_TRIU:C_TRIU + 128], rhs=nl2,
                                      start=True, stop=True), reads=["nl", "cf"], writes=[pkc])
        P.op("pe", lambda e: e.matmul(pst[:, 0:64], lhsT=self.cf[:, C_ONES:C_ONES + 128], rhs=nl2,
                                      start=True, stop=True), reads=["nl", "cf"], writes=[pkt])
        P.op("pool", lambda e: e.memset(pre[:, 0, :], 0.0), writes=["pre"])
        for c in range(1, 16):
            P.op("dve", lambda e, c=c: e.tensor_tensor(out=pre[:, c, :], in0=pre[:, c - 1, :],
                                                       in1=pst[:, (c - 1) * 4:c * 4], op=ALU.add),
                 reads=["pre", pkt], writes=["pre"])
        P.op("dve", lambda e: e.tensor_tensor(out=cumN[:, :, :].rearrange("p a b -> p (a b)"), in0=psc[:, 0:64],
                                              in1=pre[:, :, :].rearrange("p a b -> p (a b)"), op=ALU.add),
             reads=["pre", pkc], writes=["cumN"])
        P.dma("sp", "cum", [(self.x_cumst(l), cumN[:, :, :])], reads=["cumN"], writes=["blk_cum%d" % l])

    def mix(self, l):
        P, d = self.P, self.d
        self.overlay_reset()
        lam_init = 0.8 - 0.6 * math.exp(-0.3 * l)
        w = d["w_in%d" % l].rearrange("(kc p) n -> p kc n", p=128)
        wo_d = d["w_out%d" % l].rearrange("(kc p) n -> p kc n", p=128)

        def tf(s):
            v = s[:, 0:8 * 1536].rearrange("p (kc n) -> p kc n", kc=8)
            tr = []
            for kc in range(8):
                tr.append((v[:, kc, 0:256], w[:, kc, 0:256]))
                tr.append((v[:, kc, 256:512], w[:, kc, 1792:2048]))
                tr.append((v[:, kc, 512:1536], w[:, kc, 768:1792]))
            return tr
        s1, key1 = self.wload(tf, "mixw")
        wv = s1[:, 0:8 * 1536].rearrange("p (kc n) -> p kc n", kc=8)

        def tf2(s):
            v = s[:, 0:8 * 1024].rearrange("p (kc n) -> p kc n", kc=8)
            return [(v[:, kc, :], wo_d[:, kc, :]) for kc in range(8)]
        s2, key2 = self.wload(tf2, "mixwo")
        wov = s2[:, 0:8 * 1024].rearrange("p (kc n) -> p kc n", kc=8)

        gains = self.gains(l)
        sb = self.sb
        hn = sb("hn", [128, 8, TS], BF16)
        sq = sb("sq", [128, 8, TS], BF16)
        mixT, MK = sq, "sq"
        tmp2 = [sb("tmp2a", [128, TS], F32), sb("tmp2b", [128, TS], F32)]
        rstd = sb("rstd", [128, TS], F32)
        sqb = sb("qsq", [128, TS], BF16)
        stdb = sb("qstd", [128, TS], F32)
        Q = [sb("Q%d" % k, [128, TS], BF16) for k in range(2)]
        PT = [sb("PT%d" % k, [128, TS], BF16) for k in range(2)]
        mtok = sb("mtok", [128, 4, 512], BF16)
        yT = sb("yT", [128, 2, 30 + TS], BF16)
        dg = [sb("dg%d" % k, [128, 128], BF16) for k in range(2)]
        cacc = sb("cacc", [128, 2, TS], F32)
        csq = sb("csq", [128, 2, TS], F32)
        uT = sb("uT", [128, 2, TS], BF16)
        vpad = sb("vpad", [128, 4, 4, 128], BF16)
        vg = sb("vg", [128, 256], F32)
        vn = sb("vn", [128, 256], F32)
        st6 = sb("st6", [128, 8], F32)
        mv = sb("mv", [128, 4], F32)
        wT = sb("wT", [128, 4, 128], BF16)
        bsT = sb("bsT", [128, 2, 128], F32)
        NR = 4
        kvK = [sb("kvK%d" % k, [128, 1024], BF16) for k in range(NR)]
        kvV = [sb("kvV%d" % k, [128, 8, 65], BF16) for k in range(NR)]
        btab = sb("btab", [128, 32, 4], F32)
        cumi = sb("cumi", [128, 2, 16, 4], F32)
        totc = sb("totc", [128, 4], F32)
        crow = sb("crow", [128, 4, TS], BF16)
        crep = sb("crep", [128, 96], F32)
        pp = {n: sb("pp_" + n, [128, 4, 64], F32) for n in ("on", "t2", "od")}
        r1 = sb("r1", [128, 4], F32)
        r2 = sb("r2", [128, 4], F32)
        ssq = sb("ssq", [128, 4], F32)
        lamb = sb("lamb", [128, 8], F32)
        lamp = sb("lamp", [128, 2, 32], F32)
        cw = [(tmp2[0], "tmp2_0"), (tmp2[1], "tmp2_1"), (rstd, "rstd")]
        chh = [(sqb, "qsq"), (PT[0], "PT0")]
        lnm, LNM = tmp2[0], "tmp2_0"
        lnv, LNV = tmp2[1], "tmp2_1"
        sgb, SGB = rstd, "rstd"
        self.mix_bytes = self.off

        P.dma("pool", "ldg", [(wT[:, :, :], d["gmlpwT"][l])], writes=["wT"])
        if self.fused:
            gF, bF = d["gF%d" % l], d["bF%d" % l]
            flag = sb("flag", [128, 1], F32)
            P.dma("sp", "ldg2", [(bsT[:, :, :], d["gmlpbs"][l]),
                                 (cumi[:, 0, :, :], gF[0:128, 0:64].rearrange("p (s h) -> p s h", s=16)),
                                 (cumi[:, 1, :, :], bF[:, 0:64].rearrange("p (s h) -> p s h", s=16)),
                                 (totc[:, :], gF[127:128, 60:64].partition_broadcast(128)),
                                 (csq[:, :, 0:30], gF[0:128, 64:124].rearrange("p (c j) -> p c j", c=2)),
                                 (flag[:, :], d["flag"])],
                  reads=["gat%d" % l, "blk_cum%d" % l], writes=["bsT", "cumi", "totc", "csq", "flag"])
            P.op("dve", lambda e: e.tensor_scalar(out=yT[:, :, 0:30], in0=csq[:, :, 0:30], scalar1=flag[:, 0:1],
                                                  scalar2=None, op0=ALU.mult), reads=["csq", "flag"], writes=["yT"])
        else:
            flag = None
            P.dma("sp", "ldg2", [(bsT[:, :, :], d["gmlpbs"][l]),
                                 (cumi[:, :, :, :], d["cum_i"].rearrange("a p s h -> p a s h")),
                                 (totc[:, :], d["cum_i"][0, 127:128, 15, :].partition_broadcast(128)),
                                 (csq[:, :, 0:30], d["halo_i"])],
                  writes=["bsT", "cumi", "totc", "csq"])
            P.op("dve", lambda e: e.tensor_copy(out=yT[:, :, 0:30], in_=csq[:, :, 0:30]), reads=["csq"], writes=["yT"])
        P.op("dve", lambda e: e.tensor_tensor(
            out=wT[:, :, :], in0=wT[:, :, :],
            in1=self.cb[:, B_TRIU:B_TRIU + 128].unsqueeze(1).to_broadcast([128, 4, 128]), op=ALU.mult),
            reads=["wT", "cb"], writes=["wT"])
        P.op("pool", lambda e: e.memset(vpad[:, :, :, :], 0.0), writes=["vpad"])
        P.op("pool", lambda e: e.memset(crow[:, :, :], 0.0), writes=["crow"])
        P.op("dve", lambda e: e.tensor_tensor(
            out=btab[:, 0:16, :], in0=cumi[:, 0, :, :], in1=totc[:, :].unsqueeze(1).to_broadcast([128, 16, 4]),
            op=ALU.subtract), reads=["cumi", "totc"], writes=["btab"])
        if self.fused:
            P.op("dve", lambda e: e.tensor_scalar(out=btab[:, 0:16, :], in0=btab[:, 0:16, :], scalar1=flag[:, 0:1],
                                                  scalar2=None, op0=ALU.mult), reads=["btab", "flag"], writes=["btab"])
        P.op("dve", lambda e: e.tensor_copy(out=btab[:, 16:32, :], in_=cumi[:, 1, :, :]),
             reads=["cumi"], writes=["btab"])
        dl = self.small["dlam"]
        P.op("dve", lambda e: e.tensor_tensor(out=lamp[:, :, :], in0=dl[:, l, 0:4:2, :], in1=dl[:, l, 1:4:2, :],
                                              op=ALU.mult), reads=["small"], writes=["lamp"])
        P.op("dve", lambda e: e.tensor_reduce(out=lamb[:, 0:2], in_=lamp[:, :, :], axis=AX.X, op=ALU.add),
             reads=["lamp"], writes=["lamb"])
        P.op("act", lambda e: e.activation(out=lamb[:, 2:4], in_=lamb[:, 0:2], func=AF.Exp),
             reads=["lamb"], writes=["lamb"])
        P.op("dve", lambda e: e.scalar_tensor_tensor(out=lamb[:, 4:5], in0=lamb[:, 3:4], scalar=-lam_init,
                                                     in1=lamb[:, 2:3], op0=ALU.add, op1=ALU.subtract),
             reads=["lamb"], writes=["lamb"])
        nlam = lamb[:, 4:5]
        cp = self.small["convp"]
        cwt = self.small["convw"]
        gn = self.small["gmlpn"]

        scnt = 0
        ptc = 0
        rr = 0
        qc = 0
        for tt in range(NT):
            nslots = 16 + 4 * (tt + 1)
            self.hn_tile(1, tt, hn[:, :, :], "hn", sq, tmp2, rstd)
            for h in range(4):
                for sub in range(4):
                    slot = tt * 4 + sub
                    P.op("dve", lambda e, h=h, slot=slot: e.tensor_copy(
                        out=crep[:, :], in_=cumi[:, 1, slot, h:h + 1].to_broadcast([128, 96])),
                        reads=["cumi"], writes=["crep"])
                    P.op("pe", lambda e, sub=sub: e.matmul(
                        self.ps[5][0:96, sub * 128:(sub + 1) * 128], lhsT=crep[:, :],
                        rhs=self.cf[:, C_IDENT:C_IDENT + 128], start=True, stop=True),
                        reads=["crep", "cf"], writes=["ps5"])
                (c0b, c0k), (c1b, c1k), (c2b, c2k) = cw
                (h0b, h0k), (h1b, h1k) = chh
                P.op("dve", lambda e, c0b=c0b: e.tensor_scalar(out=c0b[0:96, :], in0=self.ps[5][0:96, :], scalar1=-1.0,
                                                               scalar2=None, op0=ALU.mult),
                     reads=["ps5"], writes=[c0k])
                P.op("dve", lambda e, c0b=c0b, h0b=h0b: e.tensor_copy(out=h0b[0:96, :], in_=c0b[0:96, :]),
                     reads=[c0k], writes=[h0k])
                P.op("dve", lambda e, c0b=c0b, c1b=c1b, h0b=h0b: e.tensor_tensor(
                    out=c1b[0:96, :], in0=c0b[0:96, :], in1=h0b[0:96, :], op=ALU.subtract),
                    reads=[c0k, h0k], writes=[c1k])
                P.op("dve", lambda e, c1b=c1b, h1b=h1b: e.tensor_copy(out=h1b[0:96, :], in_=c1b[0:96, :]),
                     reads=[c1k], writes=[h1k])
                P.op("dve", lambda e, c1b=c1b, c2b=c2b, h1b=h1b: e.tensor_tensor(
                    out=c2b[0:96, :], in0=c1b[0:96, :], in1=h1b[0:96, :], op=ALU.subtract),
                    reads=[c1k, h1k], writes=[c2k])
                P.op("dve", lambda e, h=h, h0b=h0b: e.tensor_copy(out=crow[0:1, h, :], in_=h0b[0:1, :]),
                     reads=[h0k], writes=["crow"])
                P.op("dve", lambda e, h=h, h1b=h1b: e.tensor_copy(out=crow[32:33, h, :], in_=h1b[32:33, :]),
                     reads=[h1k], writes=["crow"])
                P.op("dve", lambda e, h=h, c2b=c2b: e.tensor_copy(out=crow[64:65, h, :], in_=c2b[64:65, :]),
                     reads=[c2k], writes=["crow"])
            for c in range(2):
                pa, pka = self.ps[5], "ps5"
                pg, pkg = self.ps[6], "ps6"
                for kc in range(8):
                    P.op("pe", lambda e, kc=kc, c=c: e.matmul(
                        pa[:, :], lhsT=wv[:, kc, 512 + c * 128:512 + (c + 1) * 128], rhs=hn[:, kc, :],
                        start=(kc == 0), stop=(kc == 7)), reads=[key1, "hn"], writes=[pka])
                for kc in range(8):
                    P.op("pe", lambda e, kc=kc, c=c: e.matmul(
                        pg[:, :], lhsT=wv[:, kc, 768 + c * 128:768 + (c + 1) * 128], rhs=hn[:, kc, :],
                        start=(kc == 0), stop=(kc == 7)), reads=[key1, "hn"], writes=[pkg])
                P.op("act", lambda e: e.activation(out=sgb[:, :], in_=pg[:, :], func=AF.Sigmoid),
                     reads=[pkg], writes=[SGB])
                P.op("dve", lambda e, c=c: e.tensor_tensor(out=yT[:, c, 30:30 + TS], in0=sgb[:, :], in1=pa[:, :],
                                                           op=ALU.mult), reads=[SGB, pka], writes=["yT"])
            dgc = 0
            for c in range(2):
                pcv, pkcv = self.ps[5 + c], "ps%d" % (5 + c)
                for j in range(31):
                    dgb, dgk = dg[dgc % 2], "dg%d" % (dgc % 2)
                    dgc += 1
                    P.op("dve", lambda e, c=c, j=j, dgb=dgb: e.tensor_scalar(
                        out=dgb[:, :], in0=self.cb[:, B_IDENT:B_IDENT + 128], scalar1=cwt[:, l, c, j:j + 1],
                        scalar2=None, op0=ALU.mult), reads=["cb", "small"], writes=[dgk])
                    P.op("pe", lambda e, c=c, j=j, dgb=dgb, pcv=pcv: e.matmul(
                        pcv[:, :], lhsT=dgb[:, :], rhs=yT[:, c, j:j + TS], start=(j == 0), stop=(j == 30)),
                        reads=[dgk, "yT"], writes=[pkcv])
                P.op("act", lambda e, c=c, pcv=pcv: e.activation(
                    out=cacc[:, c, :], in_=pcv[:, :], func=AF.Identity, bias=cp[:, l, 0, c:c + 1], scale=1.0),
                    reads=[pkcv, "small"], writes=["cacc"])
            if tt < NT - 1:
                P.op("dve", lambda e: e.tensor_copy(out=yT[:, :, 0:30], in_=yT[:, :, TS:TS + 30]),
                     reads=["yT"], writes=["yT"])
            for c in range(2):
                pu, pku = self.ps[5], "ps5"
                for kc in range(8):
                    P.op("pe", lambda e, kc=kc, c=c: e.matmul(
                        pu[:, :], lhsT=wv[:, kc, 1024 + c * 128:1024 + (c + 1) * 128], rhs=hn[:, kc, :],
                        start=(kc == 0), stop=(kc == 7)), reads=[key1, "hn"], writes=[pku])
                P.op("act", lambda e, c=c: e.activation(out=uT[:, c, :], in_=pu[:, :], func=AF.Gelu_apprx_tanh),
                     reads=[pku], writes=["uT"])
            for sub in range(4):
                pv, pkv = self.ps[6], "ps6"
                for kc in range(8):
                    P.op("pe", lambda e, kc=kc, sub=sub: e.matmul(
                        pv[:, 0:256], lhsT=hn[:, kc, sub * 128:(sub + 1) * 128], rhs=wv[:, kc, 1280:1536],
                        start=(kc == 0), stop=(kc == 7)), reads=[key1, "hn"], writes=[pkv])
                P.op("act", lambda e: e.activation(out=vg[:, :], in_=pv[:, 0:256], func=AF.Gelu_apprx_tanh),
                     reads=[pkv], writes=["vg"])
                P.op("dve", lambda e: e.bn_stats(out=st6[:, 0:6], in_=vg[:, :]), reads=["vg"], writes=["st6"])
                P.op("dve", lambda e: e.bn_aggr(out=mv[:, 0:2], in_=st6[:, 0:6]), reads=["st6"], writes=["mv"])
                P.op("act", lambda e: e.activation(out=mv[:, 2:3], in_=mv[:, 1:2], func=AF.Sqrt, bias=EPS, scale=1.0),
                     reads=["mv"], writes=["mv"])
                P.op("dve", lambda e: e.reciprocal(out=mv[:, 3:4], in_=mv[:, 2:3]), reads=["mv"], writes=["mv"])
                P.op("dve", lambda e: e.tensor_scalar(out=vn[:, :], in0=vg[:, :], scalar1=mv[:, 0:1], scalar2=mv[:, 3:4],
                                                      op0=ALU.subtract, op1=ALU.mult),
                     reads=["vg", "mv"], writes=["vn"])
                P.op("dve", lambda e: e.tensor_tensor(out=vn[:, :], in0=vn[:, :], in1=gn[:, l, 0, :], op=ALU.mult),
                     reads=["vn", "small"], writes=["vn"])
                vn4 = vn[:, :].rearrange("p (h e) -> p h e", h=4)
                gb4 = gn[:, l, 1, :].rearrange("p (h e) -> p h e", h=4)
                for par in range(2):
                    P.op("dve", lambda e, sub=sub, par=par, vn4=vn4, gb4=gb4: e.tensor_tensor(
                        out=vpad[:, sub, par:4:2, par * 64:(par + 1) * 64], in0=vn4[:, par:4:2, :],
                        in1=gb4[:, par:4:2, :], op=ALU.add),
                        reads=["vn", "small"], writes=["vpad"])
            for h8 in range(8):
                grp = h8 // 4
                h = h8 % 4
                c0 = (0 if grp == 0 else 256) + h * 64
                Qb, Qk = Q[qc % 2], "Q%d" % (qc % 2)
                qc += 1
                self.qk_head(wv, key1, c0, hn, "hn", B_BLK32 if grp == 0 else B_BLK64,
                             1.0 / 32 if grp == 0 else 1.0 / 64,
                             gains[0:64, 0:1] if grp == 0 else gains[0:64, 2:3],
                             Qb[0:64, :], Qk, sqb, stdb)
                nmaps = 2 if grp == 0 else 1
                Kd = 32 if grp == 0 else 64
                first = [True, True]
                nchunks = (nslots + 7) // 8
                for ci in range(nchunks):
                    ns = min(8, nslots - ci * 8)
                    r = rr % NR
                    rr += 1
                    kK, kV = "kvK%d" % r, "kvV%d" % r
                    P.dma("sp", "kv%d" % r,
                          [(kvK[r][0:64, 0:ns * 128], self.x_kld(l, h8, ci, ns)),
                           (kvV[r][:, 0:ns, :], self.x_vld(l, h8, ci, ns))],
                          reads=self.x_reads(l, ci), writes=[kK, kV])
                    if self.fused and ci < 2:
                        P.op("dve", lambda e, r=r, ns=ns: e.tensor_scalar(
                            out=kvV[r][:, 0:ns, :], in0=kvV[r][:, 0:ns, :], scalar1=flag[:, 0:1], scalar2=None,
                            op0=ALU.mult), reads=[kV, "flag"], writes=[kV])
                    for m in range(nmaps):
                        r0 = m * 32
                        ob, okk = self.ps[3 + m], "ps%d" % (3 + m)
                        for w_ in range(ns):
                            c = ci * 8 + w_
                            a = c - (nslots - 4)
                            q0 = a * 128 if a > 0 else 0
                            sbk = scnt % 3
                            scnt += 1
                            S, Sk = self.ps[sbk], "ps%d" % sbk
                            diag = a >= 0
                            last_qk = not (grp == 1 or diag)
                            P.op("pe", lambda e, S=S, w_=w_, r0=r0, q0=q0, r=r, Qb=Qb, Kd=Kd, last_qk=last_qk: e.matmul(
                                S[:, q0:TS], lhsT=kvK[r][r0:r0 + Kd, w_ * 128:(w_ + 1) * 128],
                                rhs=Qb[r0:r0 + Kd, q0:TS], start=True, stop=last_qk),
                                reads=[kK, Qk], writes=[Sk])
                            if grp == 1:
                                P.op("pe", lambda e, S=S, q0=q0, h=h, diag=diag: e.matmul(
                                    S[:, q0:TS], lhsT=self.cb[0:96, B_SEL96:B_SEL96 + 128], rhs=crow[0:96, h, q0:TS],
                                    start=False, stop=not diag),
                                    reads=["crow", "cb"], writes=[Sk])
                            if diag:
                                P.op("pe", lambda e, S=S, q0=q0, a=a: e.matmul(
                                    S[:, q0:TS], lhsT=self.cb[:, B_IDENT:B_IDENT + 128],
                                    rhs=self.cb[:, B_MASK + a * 512 + q0:B_MASK + (a + 1) * 512],
                                    start=False, stop=True),
                                    reads=["cb"], writes=[Sk])
                            ptb, ptk = PT[ptc % 2], "PT%d" % (ptc % 2)
                            ptc += 1
                            if grp == 1:
                                P.op("act", lambda e, S=S, q0=q0, ptb=ptb, c=c, h=h: e.activation(
                                    out=ptb[:, q0:TS], in_=S[:, q0:TS], func=AF.Exp, bias=btab[:, c, h:h + 1],
                                    scale=1.0), reads=[Sk, "btab"], writes=[ptk])
                            else:
                                P.op("act", lambda e, S=S, q0=q0, ptb=ptb: e.activation(
                                    out=ptb[:, q0:TS], in_=S[:, q0:TS], func=AF.Exp),
                                    reads=[Sk], writes=[ptk])
                            for qs in range(max(a, 0), 4):
                                P.op("pe", lambda e, ob=ob, qs=qs, ptb=ptb, r=r, w_=w_, st=first[m], c=c: e.matmul(
                                    ob[:, qs * 65:(qs + 1) * 65], lhsT=ptb[:, qs * 128:(qs + 1) * 128],
                                    rhs=kvV[r][:, w_, :], start=st, stop=(c == nslots - 1), skip_group_check=True),
                                    reads=[ptk, kV], writes=[okk])
                                first[m] = False
                o1 = self.ps[3][:, 0:260].rearrange("p (q e) -> p q e", q=4)
                o2 = self.ps[4][:, 0:260].rearrange("p (q e) -> p q e", q=4)
                if grp == 0:
                    P.op("dve", lambda e, o1=o1: e.reciprocal(out=r1[:, :], in_=o1[:, :, 64]), reads=["ps3"], writes=["r1"])
                    P.op("dve", lambda e, o2=o2: e.reciprocal(out=r2[:, :], in_=o2[:, :, 64]), reads=["ps4"], writes=["r2"])
                    P.op("dve", lambda e: e.tensor_scalar(out=r2[:, :], in0=r2[:, :], scalar1=nlam, scalar2=None,
                                                          op0=ALU.mult), reads=["r2", "lamb"], writes=["r2"])
                    P.op("dve", lambda e, o1=o1: e.tensor_tensor(
                        out=pp["on"][:, :, :], in0=o1[:, :, 0:64], in1=r1[:, :].unsqueeze(2).to_broadcast([128, 4, 64]),
                        op=ALU.mult), reads=["ps3", "r1"], writes=["pp_on"])
                    P.op("dve", lambda e, o2=o2: e.tensor_tensor(
                        out=pp["t2"][:, :, :], in0=o2[:, :, 0:64], in1=r2[:, :].unsqueeze(2).to_broadcast([128, 4, 64]),
                        op=ALU.mult), reads=["ps4", "r2"], writes=["pp_t2"])
                    P.op("dve", lambda e: e.tensor_tensor(out=pp["od"][:, :, :], in0=pp["on"][:, :, :],
                                                          in1=pp["t2"][:, :, :], op=ALU.add),
                         reads=["pp_on", "pp_t2"], writes=["pp_od"])
                    P.op("dve", lambda e: e.tensor_tensor(out=pp["t2"][:, :, :], in0=pp["od"][:, :, :],
                                                          in1=pp["od"][:, :, :], op=ALU.mult),
                         reads=["pp_od"], writes=["pp_t2"])
                    P.op("dve", lambda e: e.tensor_reduce(out=ssq[:, :], in_=pp["t2"][:, :, :], axis=AX.X, op=ALU.add),
                         reads=["pp_t2"], writes=["ssq"])
                    P.op("act", lambda e: e.activation(out=ssq[:, :], in_=ssq[:, :], func=AF.Sqrt, bias=EPS,
                                                       scale=1.0 / 64), reads=["ssq"], writes=["ssq"])
                    P.op("dve", lambda e: e.reciprocal(out=ssq[:, :], in_=ssq[:, :]), reads=["ssq"], writes=["ssq"])
                    P.op("dve", lambda e: e.scalar_tensor_tensor(
                        out=pp["on"][:, :, :], in0=pp["od"][:, :, :], scalar=(1.0 - lam_init),
                        in1=ssq[:, :].unsqueeze(2).to_broadcast([128, 4, 64]), op0=ALU.mult, op1=ALU.mult),
                        reads=["pp_od", "ssq"], writes=["pp_on"])
                    P.op("dve", lambda e, h=h: e.tensor_tensor(
                        out=mtok[:, :, h * 64:(h + 1) * 64], in0=pp["on"][:, :, :],
                        in1=self.small["doutg"][:, l, :].unsqueeze(1).to_broadcast([128, 4, 64]), op=ALU.mult),
                        reads=["pp_on", "small"], writes=["mtok"])
                else:
                    P.op("dve", lambda e, o1=o1: e.reciprocal(out=r1[:, :], in_=o1[:, :, 64]), reads=["ps3"], writes=["r1"])
                    P.op("dve", lambda e, o1=o1, h=h: e.tensor_tensor(
                        out=mtok[:, :, 256 + h * 64:256 + (h + 1) * 64], in0=o1[:, :, 0:64],
                        in1=r1[:, :].unsqueeze(2).to_broadcast([128, 4, 64]), op=ALU.mult),
                        reads=["ps3", "r1"], writes=["mtok"])
            P.op("dve", lambda e: e.tensor_tensor(out=csq[:, :, :], in0=cacc[:, :, :], in1=cacc[:, :, :], op=ALU.mult),
                 reads=["cacc"], writes=["csq"])
            pm, pkm = self.ps[5], "ps5"
            pq, pkq = self.ps[6], "ps6"
            for c in range(2):
                P.op("pe", lambda e, c=c: e.matmul(pm[:, :], lhsT=self.cf[:, C_ONES:C_ONES + 128], rhs=cacc[:, c, :],
                                                   start=(c == 0), stop=(c == 1)), reads=["cacc", "cf"], writes=[pkm])
            for c in range(2):
                P.op("pe", lambda e, c=c: e.matmul(pq[:, :], lhsT=self.cf[:, C_ONES:C_ONES + 128], rhs=csq[:, c, :],
                                                   start=(c == 0), stop=(c == 1)), reads=["csq", "cf"], writes=[pkq])
            P.op("dve", lambda e: e.tensor_scalar(out=lnm[:, :], in0=pm[:, :], scalar1=1.0 / 256, scalar2=None,
                                                  op0=ALU.mult), reads=[pkm], writes=[LNM])
            P.op("dve", lambda e: e.tensor_tensor(out=lnv[:, :], in0=lnm[:, :], in1=lnm[:, :], op=ALU.mult),
                 reads=[LNM], writes=[LNV])
            P.op("dve", lambda e: e.scalar_tensor_tensor(out=lnv[:, :], in0=pq[:, :], scalar=1.0 / 256, in1=lnv[:, :],
                                                         op0=ALU.mult, op1=ALU.subtract),
                 reads=[pkq, LNV], writes=[LNV])
            P.op("act", lambda e: e.activation(out=lnv[:, :], in_=lnv[:, :], func=AF.Sqrt, bias=EPS, scale=1.0),
                 reads=[LNV], writes=[LNV])
            P.op("dve", lambda e: e.reciprocal(out=lnv[:, :], in_=lnv[:, :]), reads=[LNV], writes=[LNV])
            for c in range(2):
                P.op("dve", lambda e, c=c: e.tensor_tensor(out=csq[:, c, :], in0=cacc[:, c, :], in1=lnm[:, :],
                                                           op=ALU.subtract), reads=["cacc", LNM], writes=["csq"])
                P.op("dve", lambda e, c=c: e.tensor_tensor(out=csq[:, c, :], in0=csq[:, c, :], in1=lnv[:, :],
                                                           op=ALU.mult), reads=["csq", LNV], writes=["csq"])
                P.op("act", lambda e, c=c: e.activation(out=mixT[:, 2 + c, :], in_=csq[:, c, :], func=AF.Silu,
                                                        scale=cp[:, l, 1, c:c + 1], bias=cp[:, l, 2, c:c + 1]),
                     reads=["csq", "small"], writes=[MK])
            for pair in range(2):
                pgm, pkgm = self.ps[5], "ps5"
                first_g = True
                for sub in range(4):
                    for hh in range(2):
                        hd = pair * 2 + hh
                        P.op("pe", lambda e, sub=sub, hd=hd, hh=hh, st=first_g: e.matmul(
                            pgm[:, sub * 128:(sub + 1) * 128], lhsT=vpad[:, sub, hd, :], rhs=wT[:, hd, :],
                            start=st, stop=(hh == 1), skip_group_check=True),
                            reads=["vpad", "wT"], writes=[pkgm])
                        first_g = False
                P.op("dve", lambda e, pair=pair: e.tensor_tensor(
                    out=csq[:, 0, :].rearrange("p (s t) -> p s t", s=4),
                    in0=pgm[:, :].rearrange("p (s t) -> p s t", s=4),
                    in1=bsT[:, pair, :].unsqueeze(1).to_broadcast([128, 4, 128]), op=ALU.add),
                    reads=[pkgm, "bsT"], writes=["csq"])
                P.op("dve", lambda e, pair=pair: e.tensor_tensor(out=mixT[:, 4 + pair, :], in0=csq[:, 0, :],
                                                                 in1=uT[:, pair, :], op=ALU.mult),
                     reads=["csq", "uT"], writes=[MK])
            for fc in range(4):
                mc = fc if fc < 2 else fc + 4
                for qs in range(4):
                    P.op("pe", lambda e, fc=fc, qs=qs: e.transpose(
                        self.ps_bf[:, qs * 128:(qs + 1) * 128], mtok[:, qs, fc * 128:(fc + 1) * 128],
                        self.cb[:, B_IDENT:B_IDENT + 128]),
                        reads=["mtok", "cb"], writes=["psbf"])
                P.op("act", lambda e, mc=mc: e.activation(out=mixT[:, mc, :], in_=self.ps_bf[:, 0:512], func=AF.Copy),
                     reads=["psbf"], writes=[MK])
            for m in range(8):
                ob, ok = self.ps[5 + m % 2], "ps%d" % (5 + m % 2)
                for mc in range(8):
                    P.op("pe", lambda e, m=m, mc=mc, ob=ob: e.matmul(
                        ob[:, :], lhsT=wov[:, mc, m * 128:(m + 1) * 128], rhs=mixT[:, mc, :],
                        start=(mc == 0), stop=(mc == 7)), reads=[key2, MK], writes=[ok])
                P.op("dve", lambda e, m=m, ob=ob, tt=tt: e.scalar_tensor_tensor(
                    out=self.hT[:, m, tt * TS:(tt + 1) * TS], in0=ob[:, :], scalar=self.mods[:, 5, m:m + 1],
                    in1=self.hT[:, m, tt * TS:(tt + 1) * TS], op0=ALU.mult, op1=ALU.add),
                    reads=[ok, "mods", "hT%d" % tt], writes=["hT%d" % tt])

    def build(self):
        self.setup()
        for st in self.stages:
            if st[0] == "ada":
                self.ada(st[1])
            elif st[0] == "ffn":
                self.ffn(st[1], st[2])
            elif st[0] == "kv":
                self.kv(st[1])
            elif st[0] == "mix":
                self.mix(st[1])
            elif st[0] == "xchg":
                self.exchange(st[1])
        self.finish()
        self.P.emit()
        return self.nc


def _prep_shared(inp):
    f = lambda a: np.ascontiguousarray(a, dtype=np.float32)
    cf, cb = _make_consts()
    sh = {"consts_f": cf, "consts_b": cb}
    sh["ada_bT"] = f(inp["ada_b"].reshape(2, 72, 128).transpose(0, 2, 1))
    sh["norm_gT"] = f(inp["norm_g"].reshape(2, 3, 8, 128).transpose(3, 0, 1, 2).reshape(128, 48))
    p = np.arange(128)
    qkg = np.zeros((128, 2, 4), np.float32)
    for l in range(2):
        qkg[:, l, 0] = inp["diff_qk_g"][l, 0][p % 32]
        qkg[:, l, 1] = inp["diff_qk_g"][l, 1][p % 32]
        qkg[:, l, 2] = inp["fox_qk_g"][l, 0][p % 64]
        qkg[:, l, 3] = inp["fox_qk_g"][l, 1][p % 64]
    sh["qkg"] = qkg
    sh["dlam"] = f(np.broadcast_to(inp["diff_lambda"][None], (128, 2, 4, 32)))
    sh["doutg"] = f(np.broadcast_to(inp["diff_out_g"][None], (128, 2, 64)))
    sh["convw"] = f(inp["conv_w"].reshape(2, 31, 2, 128).transpose(3, 0, 2, 1))
    convp = np.stack([inp["conv_b"], inp["conv_norm_g"], inp["conv_norm_b"]], axis=1)
    sh["convp"] = f(convp.reshape(2, 3, 2, 128).transpose(3, 0, 1, 2))
    gm = np.stack([inp["gmlp_norm_g"], inp["gmlp_norm_b"]], axis=1)
    sh["gmlpn"] = f(np.broadcast_to(gm[None], (128, 2, 2, 256)))
    sh["gmlpwT"] = f(inp["gmlp_ws"].transpose(0, 3, 1, 2))
    bs = inp["gmlp_bs"]
    bsl = np.zeros((2, 128, 2, 128), np.float32)
    for pair in range(2):
        for hh in range(2):
            bsl[:, hh * 64:(hh + 1) * 64, pair, :] = bs[:, pair * 2 + hh, None, :]
    sh["gmlpbs"] = bsl
    sh["foxfb"] = f(np.broadcast_to(inp["fox_fb"][None], (128, 2, 4)))
    for l in range(2):
        sh["ada_w%d" % l] = f(inp["ada_w"][l])
        sh["f1wi%d" % l] = f(inp["ffn1_w_in"][l])
        sh["f1wo%d" % l] = f(inp["ffn1_w_out"][l])
        sh["f2wi%d" % l] = f(inp["ffn2_w_in"][l])
        sh["f2wo%d" % l] = f(inp["ffn2_w_out"][l])
        sh["w_in%d" % l] = f(inp["w_in"][l])
        sh["w_out%d" % l] = f(inp["w_out"][l])
    return sh


_LAUNCHES = [
    ([("ada", 0), ("ffn", 0, 1), ("kv", 0)], {0}),
    ([("ada", 0), ("mix", 0), ("ffn", 0, 2), ("ada", 1), ("ffn", 1, 1), ("kv", 1)], {0, 1}),
    ([("ada", 1), ("mix", 1), ("ffn", 1, 2)], {1}),
]

_BASE_KEYS = ["consts_f", "consts_b", "ada_bT", "norm_gT", "qkg", "dlam", "doutg", "convw", "convp",
              "gmlpn", "gmlpwT", "gmlpbs", "foxfb"]
_W_KEYS = ["ada_w%d", "f1wi%d", "f1wo%d", "f2wi%d", "f2wo%d", "w_in%d", "w_out%d"]


def _assemble_ctx(res):
    bf = ml_dtypes.bfloat16
    outs = []
    for core in range(8):
        half = core % 2
        own = res[core]
        kT = np.zeros((8, 64, 2 * T), bf)
        vT = np.zeros((8, 32, 128, 65), bf)
        cum = np.zeros((2, 128, 16, 4), np.float32)
        halo = np.zeros((128, 2, 30), np.float32)
        kT[:, :, T:] = np.asarray(own["kT_o"]).view(bf) if np.asarray(own["kT_o"]).dtype != bf else own["kT_o"]
        vT[:, 16:] = np.asarray(own["vT_o"]).view(bf) if np.asarray(own["vT_o"]).dtype != bf else own["vT_o"]
        cum[1] = own["cum_o"]
        if half == 1:
            par = res[core - 1]
            kT[:, :, :T] = np.asarray(par["kT_o"]).view(bf) if np.asarray(par["kT_o"]).dtype != bf else par["kT_o"]
            vT[:, :16] = np.asarray(par["vT_o"]).view(bf) if np.asarray(par["vT_o"]).dtype != bf else par["vT_o"]
            cum[0] = par["cum_o"]
            halo[:] = par["halo_o"]
        outs.append(dict(kT_i=kT, vT_i=vT, cum_i=cum, halo_i=halo))
    return outs


def run_chain(inp, launches=None, dbg=None, n_launch=None):
    launches = launches or _LAUNCHES
    sh = _prep_shared(inp)
    x = np.asarray(inp["x"], np.float32)
    c = np.asarray(inp["c"], np.float32)
    h = []
    for core in range(8):
        b, half = core // 2, core % 2
        h.append(np.ascontiguousarray(x[b, half * T:(half + 1) * T, :].T))
    ctx = None
    last = None
    for li, (stages, layers) in enumerate(launches):
        if n_launch is not None and li >= n_launch:
            break
        bld = Builder(stages, layers, dbg=(dbg if (dbg and li == (n_launch or len(launches)) - 1) else None))
        nc = bld.build()
        in_maps = []
        for core in range(8):
            b = core // 2
            m = {k: sh[k] for k in _BASE_KEYS}
            for l in sorted(layers):
                for wk in _W_KEYS:
                    m[wk % l] = sh[wk % l]
            m["hin"] = h[core]
            m["cT"] = np.ascontiguousarray(c[b].reshape(8, 128).T)
            if any(s[0] == "mix" for s in stages):
                m.update(ctx[core])
            in_maps.append(m)
        res = run_bass_kernel_spmd(nc, in_maps, core_ids=list(range(8)))
        res = res.results
        h = [np.asarray(r["hout"]) for r in res]
        if any(s[0] == "kv" for s in stages):
            ctx = _assemble_ctx(res)
        last = res
    return h, last


_FUSED_STAGES = [("ada", 0), ("ffn", 0, 1), ("kv", 0), ("xchg", 0), ("mix", 0), ("ffn", 0, 2),
                 ("ada", 1), ("ffn", 1, 1), ("kv", 1), ("xchg", 1), ("mix", 1), ("ffn", 1, 2)]
CC_INC = 1
CC_PAIRS = True


def run_fused(inp, stages=None, layers=(0, 1), trace=False):
    stages = stages or _FUSED_STAGES
    sh = _prep_shared(inp)
    x = np.asarray(inp["x"], np.float32)
    c = np.asarray(inp["c"], np.float32)
    bld = Builder(stages, set(layers), fused=True, cc_inc=CC_INC, cc_pairs=CC_PAIRS)
    nc = bld.build()
    in_maps = []
    for core in range(8):
        b, half = core // 2, core % 2
        m = {k: sh[k] for k in _BASE_KEYS}
        for l in sorted(layers):
            for wk in _W_KEYS:
                m[wk % l] = sh[wk % l]
        m["hin"] = np.ascontiguousarray(x[b, half * T:(half + 1) * T, :].T)
        m["cT"] = np.ascontiguousarray(c[b].reshape(8, 128).T)
        m["flag"] = np.full((128, 1), float(half), np.float32)
        in_maps.append(m)
    res = run_bass_kernel_spmd(nc, in_maps, core_ids=list(range(8)), **({"trace": True} if trace else {}))
    return [np.asarray(r["hout"]) for r in res.results], res


MODE = "unfused"


def kernel(**inputs):
    inp = {k: np.asarray(v) for k, v in inputs.items()}
    if MODE == "fused":
        h, _ = run_fused(inp)
    else:
        h, _ = run_chain(inp)
    out = np.zeros((4, 4096, D), np.float32)
    for core in range(8):
        b, half = core // 2, core % 2
        out[b, half * T:(half + 1) * T, :] = h[core].T
    return out
```
